# Optimizing a Trainium2 kernel written in Bass

```python
import math
import jax, jax.numpy as jnp
from jax import lax
import numpy as np

D_MODEL = 2048
BATCH = 1
SEQ = 8192
DEPTH = 4

N_MIXERS = 2
N_A = (DEPTH + 1) // 2
N_B = DEPTH // 2
EPS = 1e-6
LRU_W = D_MODEL
RG_BLOCK = 128
RG_HEADS = LRU_W // RG_BLOCK
CONV_W = 4
CONV_PAD_L = 1
CONV_PAD_R = CONV_W - 1 - CONV_PAD_L
C_RG = 8.0
NA_HEAD_DIM = 128
NA_HEADS = D_MODEL // NA_HEAD_DIM
GRID_W = 64
WIN_H = 8
WIN_W = 16
D_FF = 4 * D_MODEL
PLE_DIM = 256

kernel_name = "hybrid_rglru_natten_encoder"


def rms_norm(x, g):
    xf = x.astype(jnp.float32)
    y = xf * lax.rsqrt(jnp.mean(xf * xf, axis=-1, keepdims=True) + EPS)
    return (y * g.astype(jnp.float32)).astype(x.dtype)


def depthwise_conv_centred(x, w, b):
    c = x.shape[-1]
    y = lax.conv_general_dilated(
        x, w[:, None, :].astype(x.dtype), window_strides=(1,),
        padding=[(CONV_PAD_L, CONV_PAD_R)],
        dimension_numbers=("NWC", "WIO", "NWC"), feature_group_count=c)
    return y + b.astype(x.dtype)


def _linear_combine(e1, e2):
    a1, b1 = e1
    a2, b2 = e2
    return a1 * a2, a2 * b1 + b2


def rglru_direction(xc, w_a, b_a, w_x, b_x, lam, reverse):
    bsz, s, _ = xc.shape
    xb = xc.reshape(bsz, s, RG_HEADS, RG_BLOCK)
    r = jax.nn.sigmoid(jnp.einsum("bshi,hij->bshj", xb, w_a).reshape(bsz, s, LRU_W) + b_a)
    ig = jax.nn.sigmoid(jnp.einsum("bshi,hij->bshj", xb, w_x).reshape(bsz, s, LRU_W) + b_x)
    log_a = -C_RG * r.astype(jnp.float32) * jax.nn.softplus(-lam.astype(jnp.float32))
    a = jnp.exp(log_a)
    u = jnp.sqrt(-jnp.expm1(2.0 * log_a)) * (ig * xc).astype(jnp.float32)
    _, h = lax.associative_scan(_linear_combine, (a, u), axis=1, reverse=reverse)
    return h


def rglru_mixer(h, w_in, conv_w, conv_b, ga_w, ga_b, gx_w, gx_b, lam, w_out):
    z = h @ w_in
    xr, yg = jnp.split(z, 2, axis=-1)
    yg = jax.nn.gelu(yg, approximate=True)
    xc = depthwise_conv_centred(xr, conv_w, conv_b)
    hf = rglru_direction(xc, ga_w[0], ga_b[0], gx_w[0], gx_b[0], lam[0], reverse=False)
    hb = rglru_direction(xc, ga_w[1], ga_b[1], gx_w[1], gx_b[1], lam[1], reverse=True)
    y = (hf + hb).astype(h.dtype) * yg
    return y @ w_out


def neighborhood_attention(q, k, v, rpb):
    bsz, s, nh, dh = q.shape
    rows = s // GRID_W
    kh = min(WIN_H, rows)
    kw = WIN_W
    qg = q.reshape(bsz, rows, GRID_W, nh, dh)
    kg = k.reshape(bsz, rows, GRID_W, nh, dh)
    vg = v.reshape(bsz, rows, GRID_W, nh, dh)
    cols = jnp.arange(GRID_W)
    col_start = jnp.clip(cols - kw // 2, 0, GRID_W - kw)
    col_idx = col_start[:, None] + jnp.arange(kw)[None, :]
    rel_c = col_idx - cols[:, None] + (WIN_W - 1)

    def row_block(r):
        row_start = jnp.clip(r - kh // 2, 0, rows - kh)
        q_row = lax.dynamic_index_in_dim(qg, r, axis=1, keepdims=False)
        k_rows = lax.dynamic_slice_in_dim(kg, row_start, kh, axis=1)
        v_rows = lax.dynamic_slice_in_dim(vg, row_start, kh, axis=1)
        k_win = k_rows[:, :, col_idx]
        v_win = v_rows[:, :, col_idx]
        scores = jnp.einsum("bchd,bicjhd->bhcij", q_row, k_win).astype(jnp.float32)
        rel_r = row_start + jnp.arange(kh) - r + (WIN_H - 1)
        bias = rpb[:, rel_r[:, None, None], rel_c[None, :, :]]
        scores = scores + jnp.transpose(bias, (0, 2, 1, 3)).astype(jnp.float32)[None]
        probs = jax.nn.softmax(scores.reshape(bsz, nh, GRID_W, kh * kw), axis=-1)
        probs = probs.reshape(bsz, nh, GRID_W, kh, kw).astype(v.dtype)
        return jnp.einsum("bhcij,bicjhd->bchd", probs, v_win)

    out = lax.map(row_block, jnp.arange(rows))
    out = jnp.transpose(out, (1, 0, 2, 3, 4))
    return out.reshape(bsz, s, nh * dh)


def na_mixer(h, w_qkv, rpb, w_out):
    bsz, s, _ = h.shape
    qkv = (h @ w_qkv).reshape(bsz, s, 3, NA_HEADS, NA_HEAD_DIM)
    q = qkv[:, :, 0] * (1.0 / math.sqrt(NA_HEAD_DIM))
    k = qkv[:, :, 1]
    v = qkv[:, :, 2]
    return neighborhood_attention(q, k, v, rpb) @ w_out


def sqrelu_mlp(h, w1, w2):
    a = jax.nn.relu(h @ w1)
    return (a * a) @ w2


def setup_inputs(seed: int = 0) -> dict:
    key = jax.random.key(seed)
    ks = jax.random.split(key, 24)

    def nrm(k, shape, scale):
        return jax.random.normal(k, shape, jnp.float32) * scale

    x = nrm(ks[0], (BATCH, SEQ, D_MODEL), 1.0)
    p = nrm(ks[1], (DEPTH, BATCH, SEQ, PLE_DIM), 1.0)
    norm_mix = 1.0 + nrm(ks[2], (DEPTH, D_MODEL), 0.05)
    norm_mlp = 1.0 + nrm(ks[3], (DEPTH, D_MODEL), 0.05)
    norm_ple = 1.0 + nrm(ks[4], (DEPTH, D_MODEL), 0.05)
    norm_final = 1.0 + nrm(ks[5], (D_MODEL,), 0.05)
    rg_w_in = nrm(ks[6], (N_A, D_MODEL, 2 * LRU_W), D_MODEL ** -0.5)
    rg_conv_w = nrm(ks[7], (N_A, CONV_W, LRU_W), CONV_W ** -0.5)
    rg_conv_b = nrm(ks[8], (N_A, LRU_W), 0.01)
    rg_gate_a_w = nrm(ks[9], (N_A, 2, RG_HEADS, RG_BLOCK, RG_BLOCK), RG_BLOCK ** -0.5)
    rg_gate_a_b = nrm(ks[10], (N_A, 2, LRU_W), 0.01)
    rg_gate_x_w = nrm(ks[11], (N_A, 2, RG_HEADS, RG_BLOCK, RG_BLOCK), RG_BLOCK ** -0.5)
    rg_gate_x_b = nrm(ks[12], (N_A, 2, LRU_W), 0.01)
    a_c = jax.random.uniform(ks[13], (N_A, 2, LRU_W), jnp.float32, minval=0.9, maxval=0.999)
    s_lam = a_c ** (1.0 / C_RG)
    rg_lambda = jnp.log(s_lam) - jnp.log1p(-s_lam)
    rg_w_out = nrm(ks[14], (N_A, LRU_W, D_MODEL), LRU_W ** -0.5)
    na_w_qkv = nrm(ks[15], (N_B, D_MODEL, 3 * D_MODEL), D_MODEL ** -0.5)
    na_rpb = nrm(ks[16], (N_B, NA_HEADS, 2 * WIN_H - 1, 2 * WIN_W - 1), 0.1)
    na_w_out = nrm(ks[17], (N_B, D_MODEL, D_MODEL), D_MODEL ** -0.5)
    mlp_w1 = nrm(ks[18], (DEPTH, D_MODEL, D_FF), D_MODEL ** -0.5)
    mlp_w2 = nrm(ks[19], (DEPTH, D_FF, D_MODEL), D_FF ** -0.5)
    ple_w_gate = nrm(ks[20], (DEPTH, D_MODEL, D_MODEL), D_MODEL ** -0.5)
    ple_w_proj = nrm(ks[21], (DEPTH, PLE_DIM, D_MODEL), PLE_DIM ** -0.5)
    return {
        "x": x, "p": p,
        "norm_mix": norm_mix, "norm_mlp": norm_mlp, "norm_ple": norm_ple, "norm_final": norm_final,
        "rg_w_in": rg_w_in, "rg_conv_w": rg_conv_w, "rg_conv_b": rg_conv_b,
        "rg_gate_a_w": rg_gate_a_w, "rg_gate_a_b": rg_gate_a_b,
        "rg_gate_x_w": rg_gate_x_w, "rg_gate_x_b": rg_gate_x_b,
        "rg_lambda": rg_lambda, "rg_w_out": rg_w_out,
        "na_w_qkv": na_w_qkv, "na_rpb": na_rpb, "na_w_out": na_w_out,
        "mlp_w1": mlp_w1, "mlp_w2": mlp_w2,
        "ple_w_gate": ple_w_gate, "ple_w_proj": ple_w_proj,
    }


def reference(x, p, norm_mix, norm_mlp, norm_ple, norm_final,
              rg_w_in, rg_conv_w, rg_conv_b, rg_gate_a_w, rg_gate_a_b,
              rg_gate_x_w, rg_gate_x_b, rg_lambda, rg_w_out,
              na_w_qkv, na_rpb, na_w_out,
              mlp_w1, mlp_w2, ple_w_gate, ple_w_proj):
    h = x
    for i in range(DEPTH):
        j = i // N_MIXERS
        hn = rms_norm(h, norm_mix[i])
        if i % N_MIXERS == 0:
            mix = rglru_mixer(hn, rg_w_in[j], rg_conv_w[j], rg_conv_b[j],
                              rg_gate_a_w[j], rg_gate_a_b[j], rg_gate_x_w[j], rg_gate_x_b[j],
                              rg_lambda[j], rg_w_out[j])
        else:
            mix = na_mixer(hn, na_w_qkv[j], na_rpb[j], na_w_out[j])
        h = h + mix
        h = h + sqrelu_mlp(rms_norm(h, norm_mlp[i]), mlp_w1[i], mlp_w2[i])
        gate = jax.nn.sigmoid(rms_norm(h, norm_ple[i]) @ ple_w_gate[i])
        h = h + gate * (p[i] @ ple_w_proj[i])
    return rms_norm(h, norm_final)
```

```python
import math
from contextlib import ExitStack

import numpy as np
import ml_dtypes

import concourse.bass as bass
import concourse.mybir as mybir
from concourse.bass_utils import run_bass_kernel_spmd

F32 = mybir.dt.float32
BF16 = mybir.dt.bfloat16
I32 = mybir.dt.int32
AF = mybir.ActivationFunctionType
ALU = mybir.AluOpType

NCORES = 8
D = 2048
SEQ = 8192
T = SEQ // NCORES
TT = 512
NTT = T // TT
NKT = D // 128
DFF = 8192
PLE = 256
DEPTH = 4
SW = 256
NSLOT = 4
EPS = 1e-6
NEG = -30000.0
RING = 10
GRP = 4
NGRP = NKT // GRP

FUSED = True
import os
PHASES = set(os.environ.get("KPHASES", "mix,mlp,ple").split(","))


class Op:
    __slots__ = ("eng", "fn", "deps", "sig", "ticket", "is_dma", "dsem", "dval", "ringwait", "plain_inc", "idx")

    def __init__(self, eng, fn, is_dma):
        self.eng = eng
        self.fn = fn
        self.deps = ()
        self.sig = False
        self.ticket = None
        self.is_dma = is_dma
        self.dsem = None
        self.dval = None
        self.ringwait = None
        self.plain_inc = False


class Buf:
    __slots__ = ("w", "r", "name")

    def __init__(self, name=""):
        self.w = None
        self.r = {}
        self.name = name


class Sched:
    ENGS = ("pe", "act", "dve", "pool", "sp")

    def __init__(self, nc, stack):
        self.nc = nc
        self.streams = {e: [] for e in self.ENGS}
        self.esem = {e: stack.enter_context(nc.semaphore("es_" + e)) for e in ("pe", "act", "dve", "pool")}
        self.nring = {"sp": RING, "pool": 4}
        self.ring = {q: [stack.enter_context(nc.semaphore("dq_%s%d" % (q, i))) for i in range(self.nring[q])]
                     for q in ("sp", "pool")}
        self.dma_n = {"sp": 0, "pool": 0}
        self.stack = stack
        self.n_cc = 0
        self.n_ops = 0
        self.last = {e: None for e in self.ENGS}
        self.dmas_since_barrier = []
        self.barrier_deps = None
        self.passed = set()

    def add(self, eng, fn, reads=(), writes=(), dma=False, nobarrier=False, cc=False):
        op = Op(eng, fn, dma or cc)
        op.idx = self.n_ops
        self.n_ops += 1
        deps = set()
        for b in reads:
            if b.w is not None:
                deps.add(b.w)
        for b in writes:
            if b.w is not None:
                deps.add(b.w)
            deps.update(b.r.values())
        if self.barrier_deps is not None and not nobarrier and eng not in self.passed:
            deps.update(self.barrier_deps)
            self.passed.add(eng)
        for d in deps:
            if not d.is_dma:
                d.sig = True
        op.deps = tuple(sorted(deps, key=lambda d: -d.idx))
        for b in reads:
            b.r[(eng if not op.is_dma else ("dma", id(op)))] = op
        for b in writes:
            b.w = op
            b.r = {}
        if cc:
            sem = self.stack.enter_context(self.nc.semaphore("cc%d" % self.n_cc))
            self.n_cc += 1
            op.dsem, op.dval, op.plain_inc = sem, 1, True
            self.dmas_since_barrier.append(op)
        elif dma:
            i = self.dma_n[eng]
            self.dma_n[eng] += 1
            nr = self.nring[eng]
            sem = self.ring[eng][i % nr]
            op.dsem = sem
            op.dval = 16 * (i // nr + 1)
            if i >= nr:
                op.ringwait = (sem, op.dval - 16)
            self.dmas_since_barrier.append(op)
        else:
            self.last[eng] = op
        self.streams[eng].append(op)
        return op

    def barrier(self):
        deps = set()
        for e in ("pe", "act", "dve", "pool"):
            if self.last[e] is not None:
                deps.add(self.last[e])
                self.last[e].sig = True
        deps.update(self.dmas_since_barrier)
        self.dmas_since_barrier = []
        self.barrier_deps = deps
        self.passed = set()

    def emit(self, block):
        for e in ("pe", "act", "dve", "pool"):
            n = 0
            for op in self.streams[e]:
                if not op.is_dma and op.sig:
                    n += 1
                    op.ticket = n
        if os.environ.get("KDBG"):
            print("TICKETS", {e: max([op.ticket or 0 for op in self.streams[e] if not op.is_dma] + [0]) for e in ("pe", "act", "dve", "pool")}, "DMA", dict(self.dma_n))
        esem = self.esem

        def run(eng_name, e):
            waited = {}
            for op in self.streams[eng_name]:
                waits = []
                if op.ringwait is not None:
                    waits.append(op.ringwait)
                for d in op.deps:
                    if d.is_dma:
                        waits.append((d.dsem, d.dval))
                    elif d.eng == "pe" and eng_name == "pe" and not op.is_dma:
                        continue
                    else:
                        waits.append((esem[d.eng], d.ticket))
                for s, v in waits:
                    k = id(s)
                    if waited.get(k, 0) < v:
                        e.wait_ge(s, v)
                        waited[k] = v
                inst = op.fn(e)
                if op.is_dma:
                    if op.plain_inc:
                        inst.then_inc(op.dsem)
                    else:
                        inst.then_inc(op.dsem, 16)
                elif op.sig:
                    inst.then_inc(esem[eng_name], 1)

        @block.tensor
        def _(e):
            run("pe", e)

        @block.scalar
        def _(e):
            run("act", e)

        @block.vector
        def _(e):
            run("dve", e)

        @block.gpsimd
        def _(e):
            run("pool", e)

        @block.sync
        def _(e):
            run("sp", e)


def _vec16(v):
    return np.ascontiguousarray(np.asarray(v, np.float32).reshape(NKT, 128).T)


P_NORM = 0
P_RG = 13 * 16
RG_NV = 11
NPAR = P_RG + 2 * RG_NV * 16


def _build_params(inp):
    cols = []
    for name in ("norm_mix", "norm_mlp", "norm_ple"):
        for i in range(DEPTH):
            cols.append(_vec16(inp[name][i]))
    cols.append(_vec16(inp["norm_final"]))
    for j in range(2):
        for k in range(4):
            cols.append(_vec16(inp["rg_conv_w"][j, k]))
        cols.append(_vec16(inp["rg_conv_b"][j]))
        for nm in ("rg_gate_a_b", "rg_gate_x_b", "rg_lambda"):
            for dd in range(2):
                cols.append(_vec16(inp[nm][j, dd]))
    return np.ascontiguousarray(np.concatenate(cols, axis=1))


DELTAS = (-6, -4, -2, 0, 2, 4, 6)


def _rp_tiles(rp):
    if rp == 0:
        return [1, 2, 3, 4, 5, 6]
    if rp == 7:
        return [0, 1, 2, 3, 4, 5]
    return [1, 2, 3, 4, 5]


MASK_OFF = []
_o = 0
for _rp in range(8):
    MASK_OFF.append(_o)
    _o += len(_rp_tiles(_rp))
NMASK = _o


def _build_bias_table(rpb):
    p = np.arange(128)
    kro, kc = p // 64, p % 64
    qro, qc = p // 64, p % 64
    cs = np.clip(qc - 8, 0, 48)
    relc = kc[:, None] - qc[None, :] + 15
    colok = (kc[:, None] >= cs[None, :]) & (kc[:, None] < cs[None, :] + 16)
    out = np.full((16, 128, 7, 128), NEG, np.float32)
    for di, dl in enumerate(DELTAS):
        relr = dl + kro[:, None] - qro[None, :] + 7
        ok = colok & (relr >= 0) & (relr <= 14)
        rr = np.clip(relr, 0, 14)
        cc = np.clip(relc, 0, 30)
        vals = rpb[:, rr, cc]
        out[:, :, di, :] = np.where(ok[None], vals, np.float32(NEG))
    return out


def _build_row_mask(core):
    p = np.arange(128)
    kro = p // 64
    qro = p // 64
    m = np.zeros((128, NMASK, 128), np.float32)
    for rp in range(8):
        for ti, di in enumerate(_rp_tiles(rp)):
            dl = DELTAS[di]
            kr = 16 * core + 2 * rp + dl + kro
            r = 16 * core + 2 * rp + qro
            rs = np.clip(r - 4, 0, 120)
            ok = (kr[:, None] >= 0) & (kr[:, None] <= 127) & (kr[:, None] >= rs[None, :]) & (kr[:, None] < rs[None, :] + 8)
            m[:, MASK_OFF[rp] + ti, :] = ok
    return m.astype(ml_dtypes.bfloat16)


def build_program(layers, do_final):
    nc = bass.Bass("TRN2", target_bir_lowering=False)
    st = ExitStack()
    with st:
        return _build(nc, st, layers, do_final)


def _build(nc, st, layers, do_final):
    S = Sched(nc, st)

    def dram_in(name, shape, dt=F32):
        return nc.dram_tensor(name, list(shape), dt, kind="ExternalInput").ap()

    def dram_tmp(name, shape, dt):
        return nc.dram_tensor(name, list(shape), dt)

    hT_in = dram_in("hT_in", [D, T])
    out_d = nc.dram_tensor("out", [D, T], F32, kind="ExternalOutput").ap()
    params_d = dram_in("params", [128, NPAR])
    oh_d = dram_in("onehot", [128, 24])
    idx_d = dram_in("idx", [1, 4], I32)
    W = {}
    for i in layers:
        j = i // 2
        if "ple" in PHASES:
            W["pT%d" % i] = dram_in("pT%d" % i, [PLE, T])
            W["wg_%d" % i] = dram_in("wg_%d" % i, [D, D])
            W["wp_%d" % i] = dram_in("wp_%d" % i, [PLE, D])
        if "mlp" in PHASES:
            W["w1_%d" % i] = dram_in("w1_%d" % i, [D, DFF])
            W["w2_%d" % i] = dram_in("w2_%d" % i, [DFF, D])
        if "mix" not in PHASES:
            pass
        elif i % 2 == 0:
            W["win_%d" % i] = dram_in("win_%d" % i, [D, 2 * D])
            W["wout_%d" % i] = dram_in("wout_%d" % i, [D, D])
            W["ga_%d" % i] = dram_in("ga_%d" % i, [2, 16, 128, 128])
            W["gx_%d" % i] = dram_in("gx_%d" % i, [2, 16, 128, 128])
        else:
            W["wqkv_%d" % i] = dram_in("wqkv_%d" % i, [D, 3 * D])
            W["wout_%d" % i] = dram_in("wout_%d" % i, [D, D])
            W["bias_%d" % i] = dram_in("bias_%d" % i, [16, 128, 7 * 128])
    has_na = any(i % 2 == 1 for i in layers) and "mix" in PHASES
    has_rg = any(i % 2 == 0 for i in layers) and "mix" in PHASES
    if has_na:
        mask_d = dram_in("rowmask", [128, NMASK * 128], BF16)
        qT_d = dram_tmp("qT_d", [D, T], BF16)
        kT_d = dram_tmp("kT_d", [D, T], BF16)
        v_d = dram_tmp("v_d", [T, D], BF16)
        kv_in = dram_tmp("kv_in", [4 * 2048, 256], BF16)
        kv_out = dram_tmp("kv_out", [NCORES * 4 * 2048, 256], BF16)
        halo_d = dram_tmp("halo_d", [4 * 2048, 256], BF16)
    if has_rg:
        hb_in = dram_tmp("hb_in", [128, 48], F32)
        hb_out = dram_tmp("hb_out", [NCORES * 128, 48], F32)
        cr_in = dram_tmp("cr_in", [128, 16], F32)
        cr_out = dram_tmp("cr_out", [NCORES * 128, 16], F32)

    def sb(name, cols, dt=F32):
        return st.enter_context(nc.sbuf_tensor(name, [128, cols], dt))

    hT = sb("hT", NKT * T)
    hnT = sb("hnT", NKT * T, BF16)
    slots = [sb("wslot%d" % k, NKT * SW, BF16) for k in range(NSLOT)]
    slot_bufs = [Buf("slot%d" % k) for k in range(NSLOT)]
    params = sb("params_sb", NPAR)
    ones_f = sb("ones_f", 128)
    ones_b = sb("ones_b", 128, BF16)
    eps_t = sb("eps_t", 1)
    one_t = sb("one_t", 1)
    onehot = sb("onehot_sb", 24)
    idx_sb = st.enter_context(nc.sbuf_tensor("idx_sb", [1, 4], I32))
    crg = sb("crg", 2 * 4 * 16)
    ARENA_F32 = 18432
    arena = sb("arena", ARENA_F32)
    psum = [st.enter_context(nc.psum_tensor("ps%d" % k, [128, 512], F32)) for k in range(8)]
    ps_bufs = [Buf("ps%d" % k) for k in range(8)]
    reg_prev = st.enter_context(nc.gpsimd.register("r_prev"))
    reg_next = st.enter_context(nc.gpsimd.register("r_next"))

    B_params = Buf("params")
    B_const = Buf("const")
    B_oh = Buf("oh")
    B_idx = Buf("idx")
    B_crg = Buf("crg")
    h_bufs = [[Buf("h%d_%d" % (k, t)) for t in range(NTT)] for k in range(NKT)]
    hn_bufs = [[Buf("hn%d_%d" % (k, t)) for t in range(NTT)] for k in range(NKT)]

    def hT_ap(kt, tt):
        return hT[:, kt * T + tt * TT: kt * T + (tt + 1) * TT]

    def hnT_ap(kt, tt):
        return hnT[:, kt * T + tt * TT: kt * T + (tt + 1) * TT]

    class Arena:
        def __init__(self):
            self.off = 0

        def reset(self):
            self.off = 0

        def f32(self, cols):
            a = arena[:, self.off: self.off + cols]
            self.off += cols
            assert self.off <= ARENA_F32, ("arena overflow", self.off)
            return a

        def bf16(self, cols):
            assert cols % 2 == 0
            a = arena[:, self.off: self.off + cols // 2].bitcast(BF16)
            self.off += cols // 2
            assert self.off <= ARENA_F32, ("arena overflow", self.off)
            return a

    AR = Arena()

    ring_state = {"i": 0}

    def next_bank():
        k = ring_state["i"] % 6
        ring_state["i"] += 1
        return psum[k], ps_bufs[k]

    S.add("sp", lambda e: e.dma_start(out=params[:, :], in_=params_d[:, :]), writes=[B_params], dma=True)
    S.add("sp", lambda e: e.dma_start(out=onehot[:, :], in_=oh_d[:, :]), writes=[B_oh], dma=True)
    S.add("sp", lambda e: e.dma_start(out=idx_sb[:, :], in_=idx_d[:, :]), writes=[B_idx], dma=True)
    S.add("dve", lambda e: e.memset(ones_f[:, :], 1.0), writes=[B_const])
    S.add("dve", lambda e: e.memset(ones_b[:, :], 1.0), writes=[B_const])
    S.add("dve", lambda e: e.memset(eps_t[:, :], EPS), writes=[B_const])
    S.add("dve", lambda e: e.memset(one_t[:, :], 1.0), writes=[B_const])

    snapv = {}

    def _regs(e):
        e.reg_load(reg_prev, idx_sb[0:1, 0:1])
        e.reg_load(reg_next, idx_sb[0:1, 1:2])
        snapv[id(reg_prev)] = e.snap(reg_prev)
        snapv[id(reg_next)] = e.snap(reg_next)
        return e.nop()

    S.add("pool", _regs, reads=[B_idx])
    for kt in range(NKT):
        for tt in range(NTT):
            S.add("sp", (lambda e, kt=kt, tt=tt: e.dma_start(
                out=hT_ap(kt, tt), in_=hT_in[kt * 128:(kt + 1) * 128, tt * TT:(tt + 1) * TT])),
                writes=[h_bufs[kt][tt]], dma=True)

    class WStream:
        def __init__(self):
            self.plan = []
            self.issued = 0
            self.consumed = 0

        def _issue(self, i):
            ap, nkt = self.plan[i]
            k = i % NSLOT
            dst = slots[k][:, 0:nkt * SW].rearrange("p (k n) -> p k n", k=nkt)
            src = ap.rearrange("(k p) n -> p k n", p=128)
            S.add("pool", lambda e: e.dma_start(out=dst, in_=src), writes=[slot_bufs[k]], dma=True, nobarrier=True)

        def next(self):
            i = self.consumed
            self.consumed += 1
            while self.issued < min(len(self.plan), i + NSLOT - 1):
                self._issue(self.issued)
                self.issued += 1
            k = i % NSLOT
            return slots[k], slot_bufs[k]

    WS = WStream()

    def slabs(wname, k0, kn, n0, nn):
        w = W[wname]
        return [(w[k0:k0 + kn, n0 + s * SW: n0 + (s + 1) * SW], kn // 128) for s in range(nn // SW)]

    def layer_plan(i):
        pl = []
        if "mix" not in PHASES:
            pass
        elif i % 2 == 0:
            for g in range(NGRP):
                for half in range(2):
                    pl += slabs("win_%d" % i, 0, D, g * 512 + half * SW, SW)
                    pl += slabs("win_%d" % i, 0, D, D + g * 512 + half * SW, SW)
                pl += slabs("wout_%d" % i, g * 512, 512, 0, D)
        else:
            pl += slabs("wqkv_%d" % i, 0, D, D, D)
            pl += slabs("wqkv_%d" % i, 0, D, 2 * D, D)
            pl += slabs("wqkv_%d" % i, 0, D, 0, D)
            pl += slabs("wout_%d" % i, 0, D, 0, D)
        if "mlp" in PHASES:
            for c in range(4):
                pl += slabs("w1_%d" % i, 0, D, c * D, D)
                pl += slabs("w2_%d" % i, c * D, D, 0, D)
        if "ple" in PHASES:
            pl += slabs("wg_%d" % i, 0, D, 0, D)
        return pl

    for i in layers:
        WS.plan += layer_plan(i)

    def pe_group(ps, psb, items, reads):
        n = len(items)

        def fn(e):
            inst = None
            for q, (l, r) in enumerate(items):
                inst = e.matmul(ps, lhsT=l, rhs=r, start=(q == 0), stop=(q == n - 1))
            return inst

        return S.add("pe", fn, reads=reads, writes=[psb])

    sq_ring = {"i": 0}

    def rmsnorm(gcol, out_fn=None):
        sq = [AR_sq[0], AR_sq[1]]
        sqb = B_sq
        for tt in range(NTT):
            ps, psb = psum[6], ps_bufs[6]
            for kt in range(NKT):
                q = sq_ring["i"] % 2
                sq_ring["i"] += 1
                S.add("act", (lambda e, kt=kt, tt=tt, q=q: e.activation(out=sq[q], in_=hT_ap(kt, tt), func=AF.Square)),
                      reads=[h_bufs[kt][tt]], writes=[sqb[q]])
                S.add("pe", (lambda e, kt=kt, q=q: e.matmul(ps[:, :], lhsT=ones_f[:, :], rhs=sq[q],
                                                            start=(kt == 0), stop=(kt == NKT - 1))),
                      reads=[sqb[q], B_const], writes=[psb])
            S.add("act", lambda e: e.activation(out=AR_rstd, in_=ps[:, :], func=AF.Sqrt, bias=eps_t[:, 0:1], scale=1.0 / D),
                  reads=[psb, B_const], writes=[B_rstd])
            S.add("dve", lambda e: e.reciprocal(out=AR_rstd, in_=AR_rstd), reads=[B_rstd], writes=[B_rstd])
            for kt in range(NKT):
                if out_fn is None:
                    S.add("dve", (lambda e, kt=kt, tt=tt: e.scalar_tensor_tensor(
                        out=hnT_ap(kt, tt), in0=hT_ap(kt, tt), scalar=params[:, gcol + kt: gcol + kt + 1],
                        in1=AR_rstd, op0=ALU.mult, op1=ALU.mult)),
                        reads=[h_bufs[kt][tt], B_rstd, B_params], writes=[hn_bufs[kt][tt]])
                else:
                    out_fn(kt, tt, gcol)

    def linear(nslab, nkt, rhs_fn, rhs_reads_fn, epilogue, nt0=0, pre_nt=None):
        for s in range(nslab):
            slot, sbuf_ = WS.next()
            for q in range(SW // 128):
                nt = nt0 + s * (SW // 128) + q
                if pre_nt is not None:
                    pre_nt(nt, slot, sbuf_, q)
                for tt in range(NTT):
                    ps, psb = next_bank()
                    items = [(slot[:, kt * SW + q * 128: kt * SW + (q + 1) * 128], rhs_fn(kt, tt)) for kt in range(nkt)]
                    pe_group(ps[:, :], psb, items, [sbuf_] + rhs_reads_fn(tt))
                    epilogue(nt, tt, ps, psb)

    def hn_reads(tt):
        return [hn_bufs[kt][tt] for kt in range(NKT)]

    def add_to_h(nt, tt, ps, psb):
        S.add("dve", lambda e: e.tensor_tensor(out=hT_ap(nt, tt), in0=hT_ap(nt, tt), in1=ps[:, :], op=ALU.add),
              reads=[psb, h_bufs[nt][tt]], writes=[h_bufs[nt][tt]])

    def h_halo_exchange():
        stg = AR_hstg
        for kt in range(NKT):
            S.add("act", (lambda e, kt=kt: e.activation(out=stg[:, kt * 3: kt * 3 + 2], in_=hT[:, kt * T: kt * T + 2], func=AF.Copy)),
                  reads=[h_bufs[kt][0]], writes=[B_hstg])
            S.add("act", (lambda e, kt=kt: e.activation(out=stg[:, kt * 3 + 2: kt * 3 + 3], in_=hT[:, kt * T + T - 1: kt * T + T], func=AF.Copy)),
                  reads=[h_bufs[kt][1]], writes=[B_hstg])
        B_hbin = Buf()
        B_hbout = Buf()
        S.add("pool", lambda e: e.dma_start(out=hb_in[:, :], in_=stg), reads=[B_hstg], writes=[B_hbin], dma=True)
        S.add("pool", lambda e: e.collective_compute("AllGather", ALU.bypass, replica_groups=[list(range(NCORES))],
                                                     ins=[hb_in.ap().opt()], outs=[hb_out.ap().opt()]),
              reads=[B_hbin], writes=[B_hbout], cc=True)
        S.add("pool", lambda e: e.dma_start(out=AR_hgat.rearrange("p (r c) -> p r c", r=NCORES),
                                            in_=hb_out.ap().rearrange("(r p) c -> p r c", p=128)),
              reads=[B_hbout], writes=[B_hgat], dma=True)
        for which, base in ((0, 0), (1, 8)):
            acc = AR_hsel[:, which * 48:(which + 1) * 48]
            for r in range(NCORES):
                if r == 0:
                    S.add("dve", (lambda e, acc=acc, base=base: e.tensor_scalar(
                        out=acc, in0=AR_hgat[:, 0:48], scalar1=onehot[:, base: base + 1], scalar2=None, op0=ALU.mult)),
                        reads=[B_hgat, B_oh], writes=[B_hsel])
                else:
                    S.add("dve", (lambda e, acc=acc, base=base, r=r: e.scalar_tensor_tensor(
                        out=acc, in0=AR_hgat[:, r * 48:(r + 1) * 48], scalar=onehot[:, base + r: base + r + 1],
                        in1=acc, op0=ALU.mult, op1=ALU.add)),
                        reads=[B_hgat, B_oh, B_hsel], writes=[B_hsel])
        hh = AR_hhalo.rearrange("p (k c) -> p k c", c=4)
        selp = AR_hsel[:, 0:48].rearrange("p (k c) -> p k c", c=3)
        seln = AR_hsel[:, 48:96].rearrange("p (k c) -> p k c", c=3)
        S.add("dve", lambda e: e.memset(AR_hhalo, 0.0), writes=[B_hhalo])
        S.add("dve", lambda e: e.tensor_copy(out=hh[:, :, 0:1], in_=selp[:, :, 2:3]), reads=[B_hsel], writes=[B_hhalo])
        S.add("dve", lambda e: e.tensor_copy(out=hh[:, :, 1:3], in_=seln[:, :, 0:2]), reads=[B_hsel], writes=[B_hhalo])

    def rg_mixer(i):
        j = i // 2
        pb = P_RG + j * RG_NV * 16

        def pcol(v, kt):
            return params[:, pb + v * 16 + kt: pb + v * 16 + kt + 1]

        gmix = P_NORM + i * 16
        cs = crg[:, j * 64: j * 64 + 64]
        lam = params[:, pb + 9 * 16: pb + 11 * 16]
        S.add("act", lambda e: e.activation(out=cs[:, 0:32], in_=lam, func=AF.Exp, scale=-1.0), reads=[B_params], writes=[B_crg])
        S.add("act", lambda e: e.activation(out=cs[:, 0:32], in_=cs[:, 0:32], func=AF.Ln, bias=one_t[:, 0:1], scale=1.0),
              reads=[B_crg, B_const], writes=[B_crg])
        S.add("dve", lambda e: e.tensor_scalar(out=cs[:, 32:64], in0=cs[:, 0:32], scalar1=-16.0, scalar2=None, op0=ALU.mult),
              reads=[B_crg], writes=[B_crg])
        S.add("dve", lambda e: e.tensor_scalar(out=cs[:, 0:32], in0=cs[:, 0:32], scalar1=-8.0, scalar2=None, op0=ALU.mult),
              reads=[B_crg], writes=[B_crg])

        for m, (nm, dd) in enumerate((("ga_%d" % i, 0), ("gx_%d" % i, 0), ("ga_%d" % i, 1), ("gx_%d" % i, 1))):
            S.add("pool", (lambda e, m=m, nm=nm, dd=dd: e.dma_start(
                out=AR_gw[:, m * 2048:(m + 1) * 2048].rearrange("p (h j) -> p h j", h=16),
                in_=W[nm][dd].rearrange("h i j -> i h j"))), writes=[B_gw], dma=True)

        ps, psb = psum[7], ps_bufs[7]
        for kt in range(NKT):
            S.add("act", (lambda e, kt=kt: e.activation(out=AR_hsq[:, kt * 4:(kt + 1) * 4], in_=AR_hhalo[:, kt * 4:(kt + 1) * 4], func=AF.Square)),
                  reads=[B_hhalo], writes=[B_hsq])
        def hn_halo_mm(e):
            inst = None
            for kt in range(NKT):
                inst = e.matmul(ps[:, 0:4], lhsT=ones_f[:, :], rhs=AR_hsq[:, kt * 4:(kt + 1) * 4], start=(kt == 0), stop=(kt == NKT - 1))
            return inst
        S.add("pe", hn_halo_mm, reads=[B_hsq, B_const], writes=[psb])
        S.add("act", lambda e: e.activation(out=AR_hrstd, in_=ps[:, 0:4], func=AF.Sqrt, bias=eps_t[:, 0:1], scale=1.0 / D),
              reads=[psb, B_const], writes=[B_hrstd])
        S.add("dve", lambda e: e.reciprocal(out=AR_hrstd, in_=AR_hrstd), reads=[B_hrstd], writes=[B_hrstd])
        for kt in range(NKT):
            S.add("dve", (lambda e, kt=kt: e.scalar_tensor_tensor(
                out=AR_hnhalo[:, kt * 4:(kt + 1) * 4], in0=AR_hhalo[:, kt * 4:(kt + 1) * 4],
                scalar=params[:, gmix + kt: gmix + kt + 1], in1=AR_hrstd, op0=ALU.mult, op1=ALU.mult)),
                reads=[B_hhalo, B_hrstd, B_params], writes=[B_hnhalo])

        rmsnorm(gmix)

        for g in range(NGRP):
            for half in range(2):
                xslot, xsb = WS.next()
                yslot, ysb = WS.next()
                for q in range(2):
                    li = half * 2 + q
                    ct = g * GRP + li
                    psh, pshb = psum[7], ps_bufs[7]
                    pe_group(psh[:, 0:4], pshb,
                             [(xslot[:, kt * SW + q * 128: kt * SW + (q + 1) * 128], AR_hnhalo[:, kt * 4:(kt + 1) * 4]) for kt in range(NKT)],
                             [xsb, B_hnhalo])
                    S.add("act", lambda e: e.activation(out=AR_xr[:, 0:1], in_=psh[:, 0:1], func=AF.Copy), reads=[pshb], writes=[B_xr])
                    S.add("act", lambda e: e.activation(out=AR_xr[:, T + 1:T + 3], in_=psh[:, 1:3], func=AF.Copy), reads=[pshb], writes=[B_xr])
                    for tt in range(NTT):
                        ps, psb = next_bank()
                        pe_group(ps[:, :], psb,
                                 [(xslot[:, kt * SW + q * 128: kt * SW + (q + 1) * 128], hnT_ap(kt, tt)) for kt in range(NKT)],
                                 [xsb] + hn_reads(tt))
                        S.add("act", (lambda e, ps=ps, tt=tt: e.activation(out=AR_xr[:, 1 + tt * TT: 1 + (tt + 1) * TT], in_=ps[:, :], func=AF.Copy)),
                              reads=[psb], writes=[B_xr])
                    for tt in range(NTT):
                        ps, psb = next_bank()
                        pe_group(ps[:, :], psb,
                                 [(yslot[:, kt * SW + q * 128: kt * SW + (q + 1) * 128], hnT_ap(kt, tt)) for kt in range(NKT)],
                                 [ysb] + hn_reads(tt))
                        S.add("act", (lambda e, ps=ps, tt=tt: e.activation(out=AR_yg[:, tt * TT:(tt + 1) * TT], in_=ps[:, :], func=AF.Gelu_apprx_tanh)),
                              reads=[psb], writes=[B_yg])
                    S.add("dve", (lambda e, ct=ct: e.tensor_scalar(out=AR_xc, in0=AR_xr[:, 0:T], scalar1=pcol(0, ct), scalar2=pcol(4, ct),
                                                                     op0=ALU.mult, op1=ALU.add)),
                          reads=[B_xr, B_params], writes=[B_xc])
                    for k in range(1, 4):
                        S.add("dve", (lambda e, ct=ct, k=k: e.scalar_tensor_tensor(out=AR_xc, in0=AR_xr[:, k:k + T], scalar=pcol(k, ct), in1=AR_xc,
                                                                                     op0=ALU.mult, op1=ALU.add)),
                              reads=[B_xr, B_params, B_xc], writes=[B_xc])
                    S.add("act", lambda e: e.activation(out=AR_xcb, in_=AR_xc, func=AF.Copy), reads=[B_xc], writes=[B_xcb])
                    y0 = AR_y0[:, li * T:(li + 1) * T]
                    for dd in range(2):
                        tts = (0, 1) if dd == 0 else (1, 0)
                        Cd = AR_C[dd][:, li * T:(li + 1) * T]
                        for n_, tt in enumerate(tts):
                            sl = slice(tt * TT, (tt + 1) * TT)
                            psa, psab = next_bank()
                            psx, psxb = next_bank()
                            gw_a = AR_gw[:, (dd * 2 + 0) * 2048 + ct * 128:(dd * 2 + 0) * 2048 + (ct + 1) * 128]
                            gw_x = AR_gw[:, (dd * 2 + 1) * 2048 + ct * 128:(dd * 2 + 1) * 2048 + (ct + 1) * 128]
                            pe_group(psa[:, :], psab, [(gw_a, AR_xcb[:, sl])], [B_gw, B_xcb])
                            pe_group(psx[:, :], psxb, [(gw_x, AR_xcb[:, sl])], [B_gw, B_xcb])
                            cc1 = crg[:, j * 64 + dd * 16 + ct: j * 64 + dd * 16 + ct + 1]
                            cc2 = crg[:, j * 64 + 32 + dd * 16 + ct: j * 64 + 32 + dd * 16 + ct + 1]
                            S.add("act", (lambda e, psa=psa, dd=dd, ct=ct: e.activation(out=AR_A, in_=psa[:, :], func=AF.Sigmoid, bias=pcol(5 + dd, ct), scale=1.0)),
                                  reads=[psab, B_params], writes=[B_A])
                            S.add("act", (lambda e, psx=psx, dd=dd, ct=ct: e.activation(out=AR_IG, in_=psx[:, :], func=AF.Sigmoid, bias=pcol(7 + dd, ct), scale=1.0)),
                                  reads=[psxb, B_params], writes=[B_IG])
                            S.add("act", (lambda e, cc2=cc2: e.activation(out=AR_U, in_=AR_A, func=AF.Exp, scale=cc2)), reads=[B_A, B_crg], writes=[B_U])
                            S.add("act", (lambda e, cc1=cc1: e.activation(out=AR_A, in_=AR_A, func=AF.Exp, scale=cc1)), reads=[B_A, B_crg], writes=[B_A])
                            S.add("dve", lambda e: e.tensor_scalar(out=AR_U, in0=AR_U, scalar1=1.0, scalar2=None, op0=ALU.min), reads=[B_U], writes=[B_U])
                            S.add("act", lambda e: e.activation(out=AR_U, in_=AR_U, func=AF.Sqrt, bias=one_t[:, 0:1], scale=-1.0), reads=[B_U, B_const], writes=[B_U])
                            S.add("dve", (lambda e, sl=sl: e.tensor_tensor(out=AR_IG, in0=AR_IG, in1=AR_xc[:, sl], op=ALU.mult)), reads=[B_IG, B_xc], writes=[B_IG])
                            S.add("dve", lambda e: e.tensor_tensor(out=AR_U, in0=AR_U, in1=AR_IG, op=ALU.mult), reads=[B_U, B_IG], writes=[B_U])
                            if dd == 0:
                                hsl, asl, usl = AR_H[:, :], AR_A[:, :], AR_U[:, :]
                                acsl = AR_IG[:, :]
                            else:
                                hsl, asl, usl = AR_H[:, ::-1], AR_A[:, ::-1], AR_U[:, ::-1]
                                acsl = AR_IG[:, ::-1]
                            hinit = 0.0 if n_ == 0 else AR_carry[:, dd * 2: dd * 2 + 1]
                            ainit = 1.0 if n_ == 0 else AR_carry[:, dd * 2 + 1: dd * 2 + 2]
                            S.add("dve", (lambda e, hsl=hsl, asl=asl, usl=usl, hinit=hinit: e.tensor_tensor_scan(
                                out=hsl, data0=asl, data1=usl, initial=hinit, op0=ALU.mult, op1=ALU.add)),
                                reads=[B_A, B_U, B_carry], writes=[B_H])
                            S.add("dve", (lambda e, acsl=acsl, asl=asl, ainit=ainit: e.tensor_tensor_scan(
                                out=acsl, data0=asl, data1=asl, initial=ainit, op0=ALU.mult, op1=ALU.min)),
                                reads=[B_A, B_U, B_carry], writes=[B_IG])
                            endc = (TT - 1) if dd == 0 else 0
                            if n_ == 0:
                                S.add("dve", (lambda e, dd=dd, endc=endc: e.tensor_copy(out=AR_carry[:, dd * 2: dd * 2 + 1], in_=AR_H[:, endc:endc + 1])),
                                      reads=[B_H], writes=[B_carry])
                                S.add("dve", (lambda e, dd=dd, endc=endc: e.tensor_copy(out=AR_carry[:, dd * 2 + 1: dd * 2 + 2], in_=AR_IG[:, endc:endc + 1])),
                                      reads=[B_IG], writes=[B_carry])
                            else:
                                S.add("dve", (lambda e, dd=dd, endc=endc, li=li: e.tensor_copy(out=AR_crs[:, li * 4 + dd * 2: li * 4 + dd * 2 + 1], in_=AR_H[:, endc:endc + 1])),
                                      reads=[B_H], writes=[B_crs])
                                S.add("dve", (lambda e, dd=dd, endc=endc, li=li: e.tensor_copy(out=AR_crs[:, li * 4 + dd * 2 + 1: li * 4 + dd * 2 + 2], in_=AR_IG[:, endc:endc + 1])),
                                      reads=[B_IG], writes=[B_crs])
                            if dd == 0:
                                S.add("dve", (lambda e, sl=sl, y0=y0: e.tensor_tensor(out=y0[:, sl], in0=AR_H, in1=AR_yg[:, sl], op=ALU.mult)),
                                      reads=[B_H, B_yg], writes=[B_y0[li]])
                            else:
                                S.add("dve", (lambda e, sl=sl: e.tensor_tensor(out=AR_H, in0=AR_H, in1=AR_yg[:, sl], op=ALU.mult)),
                                      reads=[B_H, B_yg], writes=[B_H])
                                S.add("dve", (lambda e, sl=sl, y0=y0: e.tensor_tensor(out=y0[:, sl], in0=y0[:, sl], in1=AR_H, op=ALU.add)),
                                      reads=[B_H, B_y0[li]], writes=[B_y0[li]])
                            S.add("dve", (lambda e, sl=sl, Cd=Cd: e.tensor_tensor(out=Cd[:, sl], in0=AR_IG, in1=AR_yg[:, sl], op=ALU.mult)),
                                  reads=[B_IG, B_yg], writes=[B_C[dd][li]])
            B_crin = Buf()
            B_crout = Buf()
            S.add("pool", lambda e: e.dma_start(out=cr_in[:, :], in_=AR_crs), reads=[B_crs], writes=[B_crin], dma=True)
            S.add("pool", lambda e: e.collective_compute("AllGather", ALU.bypass, replica_groups=[list(range(NCORES))],
                                                         ins=[cr_in.ap().opt()], outs=[cr_out.ap().opt()]),
                  reads=[B_crin], writes=[B_crout], cc=True)
            S.add("pool", lambda e: e.dma_start(out=AR_crg.rearrange("p (r c) -> p r c", r=NCORES),
                                                in_=cr_out.ap().rearrange("(r p) c -> p r c", p=128)),
                  reads=[B_crout], writes=[B_crgat], dma=True)
            gat = AR_crg.rearrange("p (r l c) -> p r l c", r=NCORES, c=4)
            for dd in range(2):
                order = list(range(NCORES)) if dd == 0 else list(range(NCORES - 1, -1, -1))
                st_ = AR_cst[:, 0:4]
                hin = AR_hin[:, dd * 4:(dd + 1) * 4]
                S.add("dve", lambda e: e.memset(st_, 0.0), writes=[B_cst])
                S.add("dve", (lambda e, hin=hin: e.memset(hin, 0.0)), writes=[B_hin])
                for r in order:
                    S.add("dve", (lambda e, r=r, hin=hin: e.scalar_tensor_tensor(out=hin, in0=st_, scalar=onehot[:, 16 + r: 17 + r], in1=hin,
                                                                                op0=ALU.mult, op1=ALU.add)),
                          reads=[B_cst, B_oh, B_hin], writes=[B_hin])
                    S.add("dve", (lambda e, r=r, dd=dd: e.tensor_tensor(out=st_, in0=st_, in1=gat[:, r, :, dd * 2 + 1], op=ALU.mult)),
                          reads=[B_cst, B_crgat], writes=[B_cst])
                    S.add("dve", (lambda e, r=r, dd=dd: e.tensor_tensor(out=st_, in0=st_, in1=gat[:, r, :, dd * 2], op=ALU.add)),
                          reads=[B_cst, B_crgat], writes=[B_cst])
            for li in range(GRP):
                y0 = AR_y0[:, li * T:(li + 1) * T]
                for dd in range(2):
                    Cd = AR_C[dd][:, li * T:(li + 1) * T]
                    S.add("dve", (lambda e, y0=y0, Cd=Cd, dd=dd, li=li: e.scalar_tensor_tensor(
                        out=y0, in0=Cd, scalar=AR_hin[:, dd * 4 + li: dd * 4 + li + 1], in1=y0, op0=ALU.mult, op1=ALU.add)),
                        reads=[B_C[dd][li], B_hin, B_y0[li]], writes=[B_y0[li]])
            linear(D // SW, GRP,
                   lambda kt, tt: AR_y0[:, kt * T + tt * TT: kt * T + (tt + 1) * TT],
                   lambda tt: [B_y0[k] for k in range(GRP)],
                   add_to_h)

    def na_mixer(i):
        j = i // 2
        gmix = P_NORM + i * 16
        rmsnorm(gmix)
        S.add("sp", lambda e: e.dma_start(out=AR_M, in_=mask_d[:, :]), writes=[B_M], dma=True)
        stg_i = {"i": 0}
        B_q = [Buf() for _ in range(16)]
        B_k = [Buf() for _ in range(16)]
        B_v = Buf()
        B_kvin = Buf()
        kv3 = kv_in.ap().rearrange("(s f) c -> s f c", s=4)
        kv3v = kv_in.ap().rearrange("(s t k) c -> s t (k c)", s=4, k=8)

        def stage(ps, psb, scale):
            q = stg_i["i"] % 4
            stg_i["i"] += 1
            sbf = AR_stg[q]
            S.add("act", (lambda e: e.activation(out=sbf, in_=ps[:, :], func=AF.Copy, scale=scale)), reads=[psb], writes=[B_stg[q]])
            return sbf, B_stg[q]

        def k_epi(nt, tt, ps, psb):
            sbf, sbb = stage(ps, psb, 1.0)
            S.add("sp", lambda e: e.dma_start(out=kT_d[nt * 128:(nt + 1) * 128, tt * TT:(tt + 1) * TT], in_=sbf), reads=[sbb], writes=[B_k[nt]], dma=True)
            if tt == 0:
                S.add("sp", lambda e: e.dma_start(out=kv3[0, nt * 128:(nt + 1) * 128, :], in_=sbf[:, 0:256]), reads=[sbb], writes=[B_kvin], dma=True)
            else:
                S.add("sp", lambda e: e.dma_start(out=kv3[1, nt * 128:(nt + 1) * 128, :], in_=sbf[:, 256:512]), reads=[sbb], writes=[B_kvin], dma=True)

        linear(D // SW, NKT, hnT_ap, hn_reads, k_epi)
        for s in range(D // SW):
            slot, sbuf_ = WS.next()
            for tj in range(T // 128):
                ps, psb = next_bank()
                tt = tj // 4
                items = [(hnT[:, kt * T + tj * 128: kt * T + (tj + 1) * 128], slot[:, kt * SW:(kt + 1) * SW]) for kt in range(NKT)]
                pe_group(ps[:, 0:SW], psb, items, [sbuf_] + hn_reads(tt))
                q = stg_i["i"] % 4
                stg_i["i"] += 1
                sbf = AR_stg[q]
                S.add("act", (lambda e, ps=ps, sbf=sbf: e.activation(out=sbf[:, 0:SW], in_=ps[:, 0:SW], func=AF.Copy)), reads=[psb], writes=[B_stg[q]])
                S.add("sp", (lambda e, sbf=sbf, tj=tj, s=s: e.dma_start(out=v_d[tj * 128:(tj + 1) * 128, s * SW:(s + 1) * SW], in_=sbf[:, 0:SW])),
                      reads=[B_stg[q]], writes=[B_v], dma=True)
                if tj < 2:
                    S.add("sp", (lambda e, sbf=sbf, tj=tj, s=s: e.dma_start(out=kv3v[2, tj * 128:(tj + 1) * 128, s * SW:(s + 1) * SW], in_=sbf[:, 0:SW])),
                          reads=[B_stg[q]], writes=[B_kvin], dma=True)
                if tj >= 6:
                    S.add("sp", (lambda e, sbf=sbf, tj=tj, s=s: e.dma_start(out=kv3v[3, (tj - 6) * 128:(tj - 5) * 128, s * SW:(s + 1) * SW], in_=sbf[:, 0:SW])),
                          reads=[B_stg[q]], writes=[B_kvin], dma=True)
        B_kvout = Buf()
        S.add("pool", lambda e: e.collective_compute("AllGather", ALU.bypass, replica_groups=[list(range(NCORES))],
                                                     ins=[kv_in.ap().opt()], outs=[kv_out.ap().opt()]),
              reads=[B_kvin], writes=[B_kvout], cc=True)

        def q_epi(nt, tt, ps, psb):
            sbf, sbb = stage(ps, psb, 1.0 / math.sqrt(128.0))
            S.add("sp", lambda e: e.dma_start(out=qT_d[nt * 128:(nt + 1) * 128, tt * TT:(tt + 1) * TT], in_=sbf), reads=[sbb], writes=[B_q[nt]], dma=True)

        linear(D // SW, NKT, hnT_ap, hn_reads, q_epi)

        G_k = kv_out.ap().rearrange("(r s f) c -> r s f c", r=NCORES, s=4)
        G_v = kv_out.ap().rearrange("(r s t k) c -> r s t (k c)", r=NCORES, s=4, k=8)
        B_halo = Buf()
        G_blk = kv_out.ap().rearrange("(r s a b) c -> r s a (b c)", r=NCORES, s=4, a=128)
        H_blk = halo_d.ap().rearrange("(s a b) c -> s a (b c)", s=4, a=128)
        H_k = halo_d.ap().rearrange("(s f) c -> s f c", s=4)
        H_v = halo_d.ap().rearrange("(s t k) c -> s t (k c)", s=4, k=8)

        def mk_hcopy(dsti, reg, sec):
            def f(e):
                vv = snapv[id(reg)]
                return e.dma_start(out=H_blk[dsti], in_=G_blk[bass.ds(vv, 1), sec])
            return f
        S.add("pool", mk_hcopy(0, reg_prev, 1), reads=[B_kvout], writes=[B_halo], dma=True)
        S.add("pool", mk_hcopy(1, reg_next, 0), reads=[B_kvout], writes=[B_halo], dma=True)
        S.add("pool", mk_hcopy(2, reg_prev, 3), reads=[B_kvout], writes=[B_halo], dma=True)
        S.add("pool", mk_hcopy(3, reg_next, 2), reads=[B_kvout], writes=[B_halo], dma=True)
        for jj in range(2):
            S.add("sp", (lambda e, jj=jj: e.dma_start(out=AR_VH[0][:, jj * D:(jj + 1) * D], in_=H_v[2, jj * 128:(jj + 1) * 128, :])),
                  reads=[B_halo], writes=[B_VH], dma=True)
            S.add("sp", (lambda e, jj=jj: e.dma_start(out=AR_VH[1][:, jj * D:(jj + 1) * D], in_=H_v[3, jj * 128:(jj + 1) * 128, :])),
                  reads=[B_halo], writes=[B_VH], dma=True)

        for h in range(16):
            hb = h % 2
            Qh, Kx, Vx, Bh = AR_Q[hb], AR_K[hb], AR_V[hb], AR_B[hb]
            bq, bk, bv, bb = B_Qh[hb], B_Kh[hb], B_Vh[hb], B_Bh[hb]
            rows = slice(h * 128, (h + 1) * 128)
            S.add("sp", (lambda e, Qh=Qh, rows=rows: e.dma_start(out=Qh, in_=qT_d[rows, :])), reads=[B_q[h]], writes=[bq], dma=True)
            S.add("sp", (lambda e, Kx=Kx, rows=rows: e.dma_start(out=Kx[:, 256:256 + T], in_=kT_d[rows, :])), reads=[B_k[h]], writes=[bk], dma=True)
            S.add("sp", (lambda e, Vx=Vx, rows=rows: e.dma_start(out=Vx[:, 0:8 * 128].rearrange("p (j d) -> p j d", d=128),
                                                                 in_=v_d[:, rows].rearrange("(j p) d -> p j d", p=128))),
                  reads=[B_v], writes=[bv], dma=True)
            S.add("sp", (lambda e, Bh=Bh, h=h: e.dma_start(out=Bh, in_=W["bias_%d" % i][h])), writes=[bb], dma=True)

            S.add("sp", (lambda e, Kx=Kx, rows=rows: e.dma_start(out=Kx[:, 0:256], in_=H_k[0, rows, :])), reads=[B_halo], writes=[bk], dma=True)
            S.add("sp", (lambda e, Kx=Kx, rows=rows: e.dma_start(out=Kx[:, 256 + T:512 + T], in_=H_k[1, rows, :])), reads=[B_halo], writes=[bk], dma=True)

            for rp in range(8):
                tiles = _rp_tiles(rp)
                ntile = len(tiles)
                first_ext = rp + 2 + DELTAS[tiles[0]] // 2
                pi = rp % 2
                psA, psAb = psum[pi * 2], ps_bufs[pi * 2]
                psB, psBb = psum[pi * 2 + 1], ps_bufs[pi * 2 + 1]
                qs = Qh[:, rp * 128:(rp + 1) * 128]
                for ti in range(ntile):
                    et = first_ext + ti
                    dst, dstb = (psA, psAb) if ti < 4 else (psB, psBb)
                    c0 = (ti % 4) * 128
                    S.add("pe", (lambda e, dst=dst, c0=c0, et=et, qs=qs, Kx=Kx: e.matmul(dst[:, c0:c0 + 128], lhsT=Kx[:, et * 128:(et + 1) * 128], rhs=qs,
                                                                                      start=True, stop=True)),
                          reads=[bk, bq], writes=[dstb])
                Et, Etb = AR_E[pi], B_E[pi]
                Tm, Tmb = AR_T[pi], B_T[pi]
                d0 = tiles[0]
                m0 = MASK_OFF[rp]
                for (lo, hi, src, srcb) in ((0, 4, psA, psAb), (4, ntile, psB, psBb)):
                    w = (hi - lo) * 128
                    S.add("dve", (lambda e, lo=lo, w=w, src=src, Tm=Tm, Bh=Bh, d0=d0: e.tensor_tensor(
                        out=Tm[:, lo * 128: lo * 128 + w], in0=src[:, 0:w], in1=Bh[:, (d0 + lo) * 128:(d0 + lo) * 128 + w], op=ALU.add)),
                        reads=[srcb, bb], writes=[Tmb])
                    S.add("act", (lambda e, lo=lo, w=w, Tm=Tm, Et=Et: e.activation(out=Et[:, lo * 128: lo * 128 + w], in_=Tm[:, lo * 128: lo * 128 + w], func=AF.Exp)),
                          reads=[Tmb], writes=[Etb])
                    S.add("dve", (lambda e, lo=lo, w=w, Et=Et, m0=m0: e.tensor_tensor(
                        out=Et[:, lo * 128: lo * 128 + w], in0=Et[:, lo * 128: lo * 128 + w], in1=AR_M[:, (m0 + lo) * 128:(m0 + lo) * 128 + w], op=ALU.mult)),
                        reads=[Etb, B_M], writes=[Etb])
                oi = (rp // 4) % 2
                pso, psob = psum[4 + oi], ps_bufs[4 + oi]
                psd, psdb = psum[6 + oi], ps_bufs[6 + oi]
                c0 = (rp % 4) * 128

                def pv(e, pso=pso, psd=psd, c0=c0, ntile=ntile, first_ext=first_ext, Vx=Vx, Et=Et, h=h):
                    inst = None
                    for ti in range(ntile):
                        et = first_ext + ti
                        if et < 2:
                            vt = AR_VH[0][:, et * D + h * 128: et * D + (h + 1) * 128]
                        elif et >= 10:
                            vt = AR_VH[1][:, (et - 10) * D + h * 128: (et - 10) * D + (h + 1) * 128]
                        else:
                            vt = Vx[:, (et - 2) * 128:(et - 1) * 128]
                        e.matmul(pso[:, c0:c0 + 128], lhsT=vt, rhs=Et[:, ti * 128:(ti + 1) * 128],
                                 start=(ti == 0), stop=(ti == ntile - 1))
                        inst = e.matmul(psd[:, c0:c0 + 128], lhsT=ones_b[:, :], rhs=Et[:, ti * 128:(ti + 1) * 128],
                                        start=(ti == 0), stop=(ti == ntile - 1))
                    return inst

                S.add("pe", pv, reads=[bv, Etb, B_const, B_VH], writes=[psob, psdb])
                if rp % 4 == 3:
                    tt = rp // 4
                    S.add("dve", (lambda e, psd=psd: e.reciprocal(out=AR_rden, in_=psd[:, :])), reads=[psdb], writes=[B_rden])
                    S.add("dve", (lambda e, pso=pso, h=h, tt=tt: e.tensor_tensor(out=hnT_ap(h, tt), in0=pso[:, :], in1=AR_rden, op=ALU.mult)),
                          reads=[psob, B_rden], writes=[hn_bufs[h][tt]])
        linear(D // SW, NKT, hnT_ap, hn_reads, add_to_h)

    def mlp(i):
        rmsnorm(P_NORM + (4 + i) * 16)
        rl = {"i": 0}

        for c in range(4):
            def w1_epi(nt, tt, ps, psb):
                q = rl["i"] % 2
                rl["i"] += 1
                S.add("act", (lambda e, q=q: e.activation(out=AR_relu[q], in_=ps[:, :], func=AF.Relu)), reads=[psb], writes=[B_relu[q]])
                S.add("dve", (lambda e, q=q: e.tensor_tensor(out=AR_hid[:, nt * T + tt * TT: nt * T + (tt + 1) * TT], in0=AR_relu[q], in1=AR_relu[q], op=ALU.mult)),
                      reads=[B_relu[q]], writes=[B_hid[nt][tt]])

            linear(D // SW, NKT, hnT_ap, hn_reads, w1_epi)
            linear(D // SW, NKT,
                   lambda kt, tt: AR_hid[:, kt * T + tt * TT: kt * T + (tt + 1) * TT],
                   lambda tt: [B_hid[k][tt] for k in range(NKT)],
                   add_to_h)

    def ple(i):
        rmsnorm(P_NORM + (8 + i) * 16)
        S.add("pool", lambda e: e.dma_start(out=AR_wp.rearrange("p (k n) -> p k n", k=2), in_=W["wp_%d" % i].rearrange("(k p) n -> p k n", p=128)),
              writes=[B_wp], dma=True)
        S.add("pool", lambda e: e.dma_start(out=AR_pT.rearrange("p (k n) -> p k n", k=2), in_=W["pT%d" % i].rearrange("(k p) n -> p k n", p=128)),
              writes=[B_pT], dma=True)
        rl = {"i": 0}

        def gate_epi(nt, tt, ps, psb):
            q = rl["i"] % 2
            rl["i"] += 1
            S.add("act", (lambda e, q=q: e.activation(out=AR_sg[q], in_=ps[:, :], func=AF.Sigmoid)), reads=[psb], writes=[B_sg[q]])
            ps2, ps2b = next_bank()
            pe_group(ps2[:, :], ps2b,
                     [(AR_wp[:, k * D + nt * 128: k * D + (nt + 1) * 128], AR_pT[:, k * T + tt * TT: k * T + (tt + 1) * TT]) for k in range(2)],
                     [B_wp, B_pT])
            S.add("dve", (lambda e, q=q: e.tensor_tensor(out=AR_sg[q], in0=AR_sg[q], in1=ps2[:, :], op=ALU.mult)), reads=[B_sg[q], ps2b], writes=[B_sg[q]])
            S.add("dve", (lambda e, q=q: e.tensor_tensor(out=hT_ap(nt, tt), in0=hT_ap(nt, tt), in1=AR_sg[q], op=ALU.add)),
                  reads=[B_sg[q], h_bufs[nt][tt]], writes=[h_bufs[nt][tt]])

        linear(D // SW, NKT, hnT_ap, hn_reads, gate_epi)

    def arena_common():
        nonlocal AR_sq, B_sq, AR_rstd, B_rstd
        AR.reset()
        AR_sq = [AR.f32(TT), AR.f32(TT)]
        B_sq = [Buf(), Buf()]
        AR_rstd = AR.f32(TT)
        B_rstd = Buf()

    AR_sq = B_sq = AR_rstd = B_rstd = None

    out_dmas = []
    for i in layers:
        S.barrier()
        arena_common()
        if i % 2 == 0:
            AR_hstg = AR.f32(48); B_hstg = Buf()
            AR_hgat = AR.f32(NCORES * 48); B_hgat = Buf()
            AR_hsel = AR.f32(96); B_hsel = Buf()
            AR_hhalo = AR.f32(64); B_hhalo = Buf()
            AR_hsq = AR.f32(64); B_hsq = Buf()
            AR_hrstd = AR.f32(4); B_hrstd = Buf()
            AR_hnhalo = AR.bf16(64); B_hnhalo = Buf()
            AR_gw = AR.bf16(4 * 2048); B_gw = Buf()
            AR_xr = AR.f32(T + 4); B_xr = Buf()
            AR_yg = AR.bf16(T); B_yg = Buf()
            AR_xc = AR.f32(T); B_xc = Buf()
            AR_xcb = AR.bf16(T); B_xcb = Buf()
            AR_A = AR.f32(TT); B_A = Buf()
            AR_IG = AR.f32(TT); B_IG = Buf()
            AR_U = AR.f32(TT); B_U = Buf()
            AR_H = AR.f32(TT); B_H = Buf()
            AR_carry = AR.f32(4); B_carry = Buf()
            AR_crs = AR.f32(16); B_crs = Buf()
            AR_crg = AR.f32(NCORES * 16); B_crgat = Buf()
            AR_cst = AR.f32(4); B_cst = Buf()
            AR_hin = AR.f32(8); B_hin = Buf()
            AR_y0 = AR.bf16(GRP * T); B_y0 = [Buf() for _ in range(GRP)]
            AR_C = [AR.bf16(GRP * T), AR.bf16(GRP * T)]
            B_C = [[Buf() for _ in range(GRP)] for _ in range(2)]
            if "mix" in PHASES:
                h_halo_exchange()
                rg_mixer(i)
        else:
            AR_M = AR.bf16(NMASK * 128); B_M = Buf()
            AR_stg = [AR.bf16(TT) for _ in range(4)]; B_stg = [Buf() for _ in range(4)]
            AR_Q = [AR.bf16(T) for _ in range(2)]; B_Qh = [Buf(), Buf()]
            AR_K = [AR.bf16(T + 512) for _ in range(2)]; B_Kh = [Buf(), Buf()]
            AR_V = [AR.bf16(8 * 128) for _ in range(2)]; B_Vh = [Buf(), Buf()]
            AR_VH = [AR.bf16(2 * D) for _ in range(2)]; B_VH = Buf()
            AR_B = [AR.f32(7 * 128) for _ in range(2)]; B_Bh = [Buf(), Buf()]
            AR_E = [AR.bf16(6 * 128) for _ in range(2)]; B_E = [Buf(), Buf()]
            AR_T = [AR.f32(6 * 128) for _ in range(2)]; B_T = [Buf(), Buf()]
            AR_rden = AR.f32(512); B_rden = Buf()
            if "mix" in PHASES:
                na_mixer(i)
        S.barrier()
        arena_common()
        AR_relu = [AR.bf16(TT), AR.bf16(TT)]; B_relu = [Buf(), Buf()]
        AR_hid = AR.bf16(NKT * T); B_hid = [[Buf() for _ in range(NTT)] for _ in range(NKT)]
        if "mlp" in PHASES:
            mlp(i)
        S.barrier()
        arena_common()
        AR_wp = AR.bf16(2 * D); B_wp = Buf()
        AR_pT = AR.bf16(2 * T); B_pT = Buf()
        AR_sg = [AR.f32(TT), AR.f32(TT)]; B_sg = [Buf(), Buf()]
        if "ple" in PHASES:
            ple(i)

    S.barrier()
    arena_common()
    if do_final:
        AR_o = [AR.f32(TT), AR.f32(TT)]
        B_o = [Buf(), Buf()]
        oc = {"i": 0}

        def out_fn(kt, tt, gcol):
            q = oc["i"] % 2
            oc["i"] += 1
            S.add("dve", (lambda e: e.scalar_tensor_tensor(out=AR_o[q], in0=hT_ap(kt, tt), scalar=params[:, gcol + kt: gcol + kt + 1],
                                                            in1=AR_rstd, op0=ALU.mult, op1=ALU.mult)),
                  reads=[h_bufs[kt][tt], B_rstd, B_params], writes=[B_o[q]])
            out_dmas.append(S.add("sp", (lambda e: e.dma_start(out=out_d[kt * 128:(kt + 1) * 128, tt * TT:(tt + 1) * TT], in_=AR_o[q])),
                                  reads=[B_o[q]], dma=True))

        rmsnorm(P_NORM + 12 * 16, out_fn=out_fn)
    else:
        for kt in range(NKT):
            for tt in range(NTT):
                out_dmas.append(S.add("sp", (lambda e, kt=kt, tt=tt: e.dma_start(
                    out=out_d[kt * 128:(kt + 1) * 128, tt * TT:(tt + 1) * TT], in_=hT_ap(kt, tt))),
                    reads=[h_bufs[kt][tt]], dma=True))
    fin = S.add("sp", lambda e: e.nop())
    fin.deps = tuple(fin.deps) + tuple(out_dmas)

    with nc.Block() as block:
        S.emit(block)
    return nc


_PROG_CACHE = {}


def _get_prog(layers, do_final):
    key = (tuple(layers), do_final, tuple(sorted(PHASES)))
    if key not in _PROG_CACHE:
        _PROG_CACHE[key] = build_program(list(layers), do_final)
    return _PROG_CACHE[key]


def _layer_inputs(inp, i):
    j = i // 2
    d = {}
    if "mlp" in PHASES:
        d.update({"w1_%d" % i: inp["mlp_w1"][i], "w2_%d" % i: inp["mlp_w2"][i]})
    if "ple" in PHASES:
        d.update({"wg_%d" % i: inp["ple_w_gate"][i], "wp_%d" % i: inp["ple_w_proj"][i]})
    if "mix" not in PHASES:
        pass
    elif i % 2 == 0:
        d["win_%d" % i] = inp["rg_w_in"][j]
        d["wout_%d" % i] = inp["rg_w_out"][j]
        d["ga_%d" % i] = inp["rg_gate_a_w"][j]
        d["gx_%d" % i] = inp["rg_gate_x_w"][j]
    else:
        d["wqkv_%d" % i] = inp["na_w_qkv"][j]
        d["wout_%d" % i] = inp["na_w_out"][j]
        d["bias_%d" % i] = _build_bias_table(inp["na_rpb"][j]).reshape(16, 128, 7 * 128)
    return d


def run_layers(inp, hT_cores, layers, do_final, trace=False, phases=None):
    global PHASES
    if phases is not None:
        PHASES = set(phases)
    nc = _get_prog(layers, do_final)
    params = _build_params(inp)
    shared = {"params": params}
    for i in layers:
        shared.update(_layer_inputs(inp, i))
    in_maps = []
    for c in range(NCORES):
        m = dict(shared)
        m["hT_in"] = hT_cores[c]
        oh = np.zeros((128, 24), np.float32)
        if c > 0:
            oh[:, c - 1] = 1.0
        if c < NCORES - 1:
            oh[:, 8 + c + 1] = 1.0
        oh[:, 16 + c] = 1.0
        m["onehot"] = oh
        m["idx"] = np.array([[max(c - 1, 0), min(c + 1, NCORES - 1), 0, 0]], np.int32)
        if "ple" in PHASES:
            for i in layers:
                m["pT%d" % i] = np.ascontiguousarray(inp["p"][i, 0, c * T:(c + 1) * T, :].T)
        if any(i % 2 == 1 for i in layers) and "mix" in PHASES:
            m["rowmask"] = _build_row_mask(c).reshape(128, NMASK * 128)
        in_maps.append(m)
    res = run_bass_kernel_spmd(nc, in_maps, core_ids=list(range(NCORES)), **({"trace": True} if trace else {}))
    return [np.asarray(r["out"]) for r in res.results], res


def kernel(**inputs):
    inp = {k: np.asarray(v) for k, v in inputs.items()}
    x = inp["x"][0]
    hT = [np.ascontiguousarray(x[c * T:(c + 1) * T, :].T) for c in range(NCORES)]
    if FUSED:
        hT, _ = run_layers(inp, hT, list(range(DEPTH)), True)
    else:
        for i in range(DEPTH):
            if i % 2 == 0:
                hT, _ = run_layers(inp, hT, [i], False, phases=("mix", "mlp"))
                hT, _ = run_layers(inp, hT, [i], False, phases=("ple",))
            else:
                hT, _ = run_layers(inp, hT, [i], i == DEPTH - 1, phases=("mix", "mlp", "ple"))
    out = np.concatenate([h.T for h in hT], axis=0)[None]
    return np.ascontiguousarray(out.astype(np.float32))
```
